# Optimizing a Trainium2 kernel written in Bass

```python
import jax, jax.numpy as jnp
from jax import lax
import numpy as np

D_MODEL = 1024
BATCH = 8
SEQ = 4096
DEPTH = 4

N_META = 16
MIX_WIDTH = 2 * D_MODEL
DN_HEAD_DIM = 128
DN_WIDTH = D_MODEL
DN_HEADS = DN_WIDTH // DN_HEAD_DIM
DN_CONV = 4
CHUNK = 64
SC_WIDTH = MIX_WIDTH - DN_WIDTH
SC_CONV = 3
EPS = 1e-6

SPLITS = (
    3 * DN_WIDTH,
    4 * DN_WIDTH,
    4 * DN_WIDTH + DN_HEADS,
    4 * DN_WIDTH + 2 * DN_HEADS,
    4 * DN_WIDTH + 2 * DN_HEADS + SC_WIDTH,
    4 * DN_WIDTH + 2 * DN_HEADS + 2 * SC_WIDTH,
    4 * DN_WIDTH + 2 * DN_HEADS + 3 * SC_WIDTH,
)
IN_COLS = 4 * DN_WIDTH + 2 * DN_HEADS + 4 * SC_WIDTH

kernel_name = "hymba_gdn_shortconv_hybrid"


def rmsnorm(x, gain):
    xf = x.astype(jnp.float32)
    y = xf * lax.rsqrt(jnp.mean(xf * xf, axis=-1, keepdims=True) + EPS)
    return (y * gain.astype(jnp.float32)).astype(x.dtype)


def l2norm(x):
    return x * lax.rsqrt(jnp.sum(x * x, axis=-1, keepdims=True) + EPS)


def causal_dwconv(x, w):
    K = w.shape[0]
    L = x.shape[1]
    xp = jnp.pad(x, ((0, 0), (K - 1, 0), (0, 0)))
    y = xp[:, 0:L, :] * w[0]
    for j in range(1, K):
        y = y + xp[:, j:j + L, :] * w[j]
    return y


def chunk_gated_delta_rule(q, k, v, g, beta):
    Bsz, L, H, dk = q.shape
    dv = v.shape[-1]
    N = L // CHUNK

    def to_chunks(t):
        t = t.reshape((Bsz, N, CHUNK, H) + t.shape[3:])
        return jnp.moveaxis(t, 3, 1)

    q, k, v, g, beta = (to_chunks(t) for t in (q, k, v, g, beta))
    g = jnp.cumsum(g, axis=-1)
    idx = jnp.arange(CHUNK)
    causal = idx[:, None] >= idx[None, :]
    strict = idx[:, None] > idx[None, :]
    decay = jnp.exp(jnp.where(causal, g[..., :, None] - g[..., None, :], -jnp.inf))
    kk = jnp.einsum('bhnid,bhnjd->bhnij', k, k)
    a_low = jnp.where(strict, beta[..., :, None] * kk * decay, 0.0)
    tmat = a_low + jnp.eye(CHUNK, dtype=a_low.dtype)
    u = lax.linalg.triangular_solve(tmat, v * beta[..., None], left_side=True, lower=True, unit_diagonal=True)
    w = lax.linalg.triangular_solve(tmat, k * (beta * jnp.exp(g))[..., None], left_side=True, lower=True, unit_diagonal=True)
    qk = jnp.einsum('bhnid,bhnjd->bhnij', q, k) * decay
    q_dec = q * jnp.exp(g)[..., None]
    g_last = g[..., -1]
    k_dec = k * jnp.exp(g_last[..., None] - g)[..., None]
    xs = tuple(jnp.moveaxis(t, 2, 0) for t in (u, w, qk, q_dec, k_dec, jnp.exp(g_last)))

    def step(S, inp):
        u_c, w_c, qk_c, qd_c, kd_c, gl_c = inp
        v_new = u_c - jnp.einsum('bhcd,bhde->bhce', w_c, S)
        o = jnp.einsum('bhcd,bhde->bhce', qd_c, S) + jnp.einsum('bhij,bhje->bhie', qk_c, v_new)
        S = S * gl_c[..., None, None] + jnp.einsum('bhcd,bhce->bhde', kd_c, v_new)
        return S, o

    S0 = jnp.zeros((Bsz, H, dk, dv), jnp.float32)
    _, o = lax.scan(step, S0, xs)
    o = jnp.moveaxis(o, 0, 2)
    return jnp.moveaxis(o, 1, 3).reshape(Bsz, L, H, dv)


def hybrid_layer(x, norm_g, w_in, dn_conv_w, dn_A_log, dn_dt_bias, dn_out_g, sc_conv_w, w_out):
    Bsz, L, _ = x.shape
    f32 = jnp.float32
    h = rmsnorm(x, norm_g)
    p = h @ w_in
    qkv, dn_z, dn_a, dn_b, sc_b, sc_c, sc_h, sc_z = jnp.split(p, SPLITS, axis=-1)

    qkv = jax.nn.silu(causal_dwconv(qkv, dn_conv_w)).astype(f32)
    q, k, v = jnp.split(qkv, 3, axis=-1)
    q = l2norm(q.reshape(Bsz, L, DN_HEADS, DN_HEAD_DIM)) * (DN_HEAD_DIM ** -0.5)
    k = l2norm(k.reshape(Bsz, L, DN_HEADS, DN_HEAD_DIM))
    v = v.reshape(Bsz, L, DN_HEADS, DN_HEAD_DIM)
    g = -jnp.exp(dn_A_log.astype(f32)) * jax.nn.softplus(dn_a.astype(f32) + dn_dt_bias.astype(f32))
    beta = jax.nn.sigmoid(dn_b.astype(f32))
    pad_front = (-N_META) % CHUNK
    pad_back = (-(pad_front + L)) % CHUNK

    def padt(t):
        return jnp.pad(t, ((0, 0), (pad_front, pad_back)) + ((0, 0),) * (t.ndim - 2))

    o = chunk_gated_delta_rule(padt(q), padt(k), padt(v), padt(g), padt(beta))[:, pad_front:pad_front + L]
    o = rmsnorm(o, dn_out_g) * jax.nn.silu(dn_z.astype(f32).reshape(Bsz, L, DN_HEADS, DN_HEAD_DIM))
    o_dn = o.reshape(Bsz, L, DN_WIDTH).astype(x.dtype)

    y = sc_b * causal_dwconv(sc_c * sc_h, sc_conv_w)
    o_sc = (y * jax.nn.silu(sc_z)).astype(x.dtype)

    mix = jnp.concatenate([o_dn, o_sc], axis=-1)
    return x + mix @ w_out


def setup_inputs(seed: int = 0) -> dict:
    key = jax.random.key(seed)
    ks = jax.random.split(key, 12)
    x = jax.random.normal(ks[0], (BATCH, SEQ, D_MODEL), jnp.float32)
    meta_tokens = jax.random.normal(ks[1], (N_META, D_MODEL), jnp.float32)
    norm_g = 1.0 + 0.02 * jax.random.normal(ks[2], (DEPTH, D_MODEL), jnp.float32)
    w_in = jax.random.normal(ks[3], (DEPTH, D_MODEL, IN_COLS), jnp.float32) * (D_MODEL ** -0.5)
    dn_conv_w = jax.random.normal(ks[4], (DEPTH, DN_CONV, 3 * DN_WIDTH), jnp.float32) * (DN_CONV ** -0.5)
    dn_A_log = jnp.log(jax.random.uniform(ks[5], (DEPTH, DN_HEADS), jnp.float32, 1.0, 16.0))
    dt = jnp.exp(jax.random.uniform(ks[6], (DEPTH, DN_HEADS), jnp.float32, np.log(1e-3), np.log(1e-1)))
    dn_dt_bias = dt + jnp.log(-jnp.expm1(-dt))
    dn_out_g = 1.0 + 0.02 * jax.random.normal(ks[7], (DEPTH, DN_HEAD_DIM), jnp.float32)
    sc_conv_w = jax.random.normal(ks[8], (DEPTH, SC_CONV, SC_WIDTH), jnp.float32) * (SC_CONV ** -0.5)
    w_out = jax.random.normal(ks[9], (DEPTH, MIX_WIDTH, D_MODEL), jnp.float32) * (MIX_WIDTH ** -0.5)
    final_g = 1.0 + 0.02 * jax.random.normal(ks[10], (D_MODEL,), jnp.float32)
    return {"x": x, "meta_tokens": meta_tokens, "norm_g": norm_g, "w_in": w_in,
            "dn_conv_w": dn_conv_w, "dn_A_log": dn_A_log, "dn_dt_bias": dn_dt_bias,
            "dn_out_g": dn_out_g, "sc_conv_w": sc_conv_w, "w_out": w_out, "final_g": final_g}


def reference(x, meta_tokens, norm_g, w_in, dn_conv_w, dn_A_log, dn_dt_bias, dn_out_g, sc_conv_w, w_out, final_g):
    Bsz = x.shape[0]
    meta = jnp.broadcast_to(meta_tokens[None].astype(x.dtype), (Bsz, N_META, D_MODEL))
    h = jnp.concatenate([meta, x], axis=1)
    for l in range(DEPTH):
        h = hybrid_layer(h, norm_g[l], w_in[l], dn_conv_w[l], dn_A_log[l], dn_dt_bias[l],
                         dn_out_g[l], sc_conv_w[l], w_out[l])
    return rmsnorm(h[:, N_META:], final_g)
```

```python
import numpy as np
from contextlib import ExitStack
import concourse.bass as bass
import concourse.mybir as mybir
from concourse.bass_utils import run_bass_kernel_spmd

F32 = mybir.dt.float32
BF16 = mybir.dt.bfloat16
ALU = mybir.AluOpType
AF = mybir.ActivationFunctionType

D_MODEL = 1024
N_META = 16
DEPTH = 4
EPS = 1e-6
P = 128
NBLK = 40
PL = 8 + 96 + 24 + 1 + 8 + 8 + 128
SEM_LIMIT = 4000


class Trk:
    __slots__ = ("w", "r")

    def __init__(self):
        self.w = {}
        self.r = {}


class V:
    def __init__(self, ap, trk, bank=None, gen=None):
        self.ap = ap
        self.trk = trk
        self.bank = bank
        self.gen = gen

    def _n(self, ap):
        return V(ap, self.trk, self.bank, self.gen)

    def __getitem__(self, k):
        return self._n(self.ap[k])

    def re(self, pat, **kw):
        return self._n(self.ap.rearrange(pat, **kw))

    def bc(self, shape):
        return self._n(self.ap.to_broadcast(list(shape)))

    def un(self, axis):
        return self._n(self.ap.unsqueeze(axis))

    def bitcast(self, dt):
        return self._n(self.ap.bitcast(dt))


class Bank:
    def __init__(self, v):
        self.v = v
        self.gen = 0


class KB:
    ENGS = ("pe", "act", "dve", "pool", "sp")

    def __init__(self, nc, es):
        self.nc = nc
        self.es = es
        self.q = {e: [] for e in self.ENGS}
        self.waited = {e: {} for e in self.ENGS}
        self.dma_ring = {}
        self.n_dma_sems = 0

    def sb(self, name, shape, dt):
        t = self.es.enter_context(self.nc.sbuf_tensor(name, list(shape), dt))
        return V(t[:], Trk())

    def ps(self, name, shape, dt):
        t = self.es.enter_context(self.nc.psum_tensor(name, list(shape), dt))
        return V(t[:], Trk())

    def dram(self, name, shape, dt, kind):
        t = self.nc.dram_tensor(name, list(shape), dt, kind=kind)
        return V(t.ap(), Trk())

    def _emit(self, eng, fn, reads, writes, dma=None):
        idx = len(self.q[eng])
        deps = {}

        def add(key, ev):
            old = deps.get(key)
            if old is None or ev[1] > old[1]:
                deps[key] = ev

        for v in reads:
            if v.bank is not None and v.bank.gen != v.gen:
                raise RuntimeError("stale PSUM view read")
            for key, ev in v.trk.w.items():
                add(key, ev)
            if v.bank is not None:
                for key, ev in v.trk.r.items():
                    if key != eng:
                        add(key, ev)
        for v in writes:
            if v.bank is not None and v.bank.gen != v.gen:
                raise RuntimeError("stale PSUM view write")
            for key, ev in v.trk.w.items():
                if key != eng or eng != "pe":
                    add(key, ev)
            for key, ev in v.trk.r.items():
                if key != eng or eng != "pe":
                    add(key, ev)
        waits = []
        wd = self.waited[eng]
        for key, ev in deps.items():
            if wd.get(key, -1) >= ev[1]:
                continue
            wd[key] = ev[1]
            waits.append(ev)
            if ev[0] == "cc":
                self.q[ev[2]][ev[1]]["sig"] = True
        item = {"fn": fn, "waits": waits, "sig": False}
        self.q[eng].append(item)
        if dma is None:
            key, ev = eng, ("cc", idx, eng)
        else:
            key, ev = dma
        for v in reads:
            v.trk.r[key] = ev
        for v in writes:
            v.trk.w[key] = ev
        return item

    def mm(self, out, lhsT, rhs, start=True, stop=True, **kw):
        def fn(e):
            return e.matmul(out.ap, lhsT.ap, rhs.ap, start=start, stop=stop, **kw)
        self._emit("pe", fn, [lhsT, rhs], [out])

    def tr(self, out, in_, ident):
        def fn(e):
            return e.transpose(out.ap, in_.ap, ident.ap)
        self._emit("pe", fn, [in_, ident], [out])

    def act(self, out, in_, func, bias=None, scale=1.0, eng="act"):
        rds = [in_]
        kw = {}
        if bias is not None:
            if isinstance(bias, V):
                rds.append(bias)
                kw["bias"] = bias.ap
            else:
                kw["bias"] = float(bias)
        if isinstance(scale, V):
            rds.append(scale)
            sc = scale.ap
        else:
            sc = float(scale)

        def fn(e):
            return e.activation(out.ap, in_.ap, func, scale=sc, **kw)
        self._emit(eng, fn, rds, [out])

    def tt(self, eng, out, a, b, op):
        def fn(e):
            return e.tensor_tensor(out.ap, a.ap, b.ap, op)
        self._emit(eng, fn, [a, b], [out])

    def ts(self, eng, out, a, s1, op0, s2=None, op1=None):
        rds = [a]
        s1a = s1
        if isinstance(s1, V):
            rds.append(s1)
            s1a = s1.ap
        s2a = s2
        if isinstance(s2, V):
            rds.append(s2)
            s2a = s2.ap
        if op1 is None:
            def fn(e):
                return e.tensor_single_scalar(out.ap, a.ap, s1a, op0)
        else:
            def fn(e):
                return e.tensor_scalar(out.ap, a.ap, s1a, s2a, op0, op1)
        self._emit(eng, fn, rds, [out])

    def stt(self, eng, out, a, sc, b, op0, op1):
        rds = [a, b]
        sca = sc
        if isinstance(sc, V):
            rds.append(sc)
            sca = sc.ap

        def fn(e):
            return e.scalar_tensor_tensor(out.ap, a.ap, sca, b.ap, op0, op1)
        self._emit(eng, fn, rds, [out])

    def cp(self, eng, out, a):
        if eng == "act":
            return self.act(out, a, AF.Copy)

        def fn(e):
            return e.tensor_copy(out.ap, a.ap)
        self._emit(eng, fn, [a], [out])

    def memset(self, eng, out, val):
        def fn(e):
            return e.memset(out.ap, val)
        self._emit(eng, fn, [], [out])

    def rsq(self, out, in_, eps):
        self.act(out, in_, AF.Ln, bias=eps)
        self.act(out, out, AF.Exp, scale=-0.5)

    def recip(self, out, a):
        def fn(e):
            return e.reciprocal(out.ap, a.ap)
        self._emit("dve", fn, [a], [out])

    def dma(self, q, out, in_, nsem=8):
        ring = self.dma_ring.setdefault(q, {"sems": [], "pos": 0})
        if len(ring["sems"]) < nsem:
            s = self.es.enter_context(self.nc.semaphore("dsem%d" % self.n_dma_sems))
            self.n_dma_sems += 1
            ring["sems"].append([s, 0, "d%d" % self.n_dma_sems])
        slot = ring["sems"][ring["pos"] % len(ring["sems"])]
        ring["pos"] += 1
        sem, cnt, key = slot
        slot[1] = cnt + 16

        def fn(e):
            return e.dma_start(out=out.ap, in_=in_.ap).then_inc(sem, 16)
        item = self._emit(q, fn, [in_], [out], dma=(key, ("d", cnt + 16, sem)))
        if cnt > 0 and self.waited[q].get(key, -1) < cnt:
            item["waits"].append(("d", cnt, sem))
            self.waited[q][key] = cnt
        return item

    def finish(self, final_waits):
        nc = self.nc
        self._emit("sp", None, final_waits, [])
        sems = {}
        for e in self.ENGS:
            n = 0
            for it in self.q[e]:
                if it["sig"]:
                    it["signo"] = n
                    n += 1
            nsem = max(1, (n + SEM_LIMIT - 1) // SEM_LIMIT)
            sems[e] = [self.es.enter_context(nc.semaphore("s_%s_%d" % (e, i))) for i in range(nsem)]

        def semval(e, signo):
            return sems[e][signo // SEM_LIMIT], (signo % SEM_LIMIT) + 1

        def run(ename, eng):
            for it in self.q[ename]:
                for ev in it["waits"]:
                    if ev[0] == "c":
                        raise RuntimeError("unreachable")
                    elif ev[0] == "cc":
                        s, v = semval(ev[2], self.q[ev[2]][ev[1]]["signo"])
                        eng.wait_ge(s, v)
                    else:
                        eng.wait_ge(ev[2], ev[1])
                if it["fn"] is None:
                    continue
                ins = it["fn"](eng)
                if it["sig"]:
                    s, v = semval(ename, it["signo"])
                    ins.then_inc(s, 1)

        with nc.Block() as block:
            @block.tensor
            def _(e):
                run("pe", e)

            @block.scalar
            def _(e):
                run("act", e)

            @block.vector
            def _(e):
                run("dve", e)

            @block.gpsimd
            def _(e):
                run("pool", e)

            @block.sync
            def _(e):
                run("sp", e)


def build_program(T, groups, NL, first, final):
    nc = bass.Bass("TRN2", target_bir_lowering=False)
    NTMAX = max(nt for _, nt in groups)
    GM = NTMAX * P
    NP_ = NL * PL + 8
    with ExitStack() as es:
        k = KB(nc, es)
        xin = k.dram("xin", [T * P, D_MODEL], F32, "ExternalInput")
        wall = k.dram("wall", [NL, NBLK, P, 2048], F32, "ExternalInput")
        params_d = k.dram("params", [P, NP_], F32, "ExternalInput")
        consts_d = k.dram("consts", [P, 6, P], F32, "ExternalInput")
        n_out_tiles = T - 1 if final else T
        out_d = k.dram("out", [n_out_tiles * P, D_MODEL], F32, "ExternalOutput")
        NCH = 5
        wbf = [[k.dram("wbf_%d_%d" % (l, c), [NBLK // NCH, P, 2048], BF16, "Internal") for c in range(NCH)]
               for l in range(NL)]

        prm = k.sb("prm", [P, NP_], F32)
        cst = k.sb("cst", [P, 6, P], F32)
        identb = k.sb("identb", [P, P], BF16)
        ones_mean = k.sb("ones_mean", [P, P], BF16)
        ones_h = k.sb("ones_h", [P, P], BF16)
        ones_q = k.sb("ones_q", [P, P], BF16)
        ones_k = k.sb("ones_k", [P, P], BF16)
        wabb = k.sb("wabb", [P, NL, 8, 16], BF16)
        negA = k.sb("negA", [P, NL, 8], F32)
        ogh = k.sb("ogh", [P, NL], F32)
        scwh = k.sb("scwh", [P, NL, 24], F32)
        xT = k.sb("xT", [P, 8, GM], F32)
        hT = k.sb("hT", [P, 8, GM], BF16)
        rstd = k.sb("rstd", [P, GM], F32)
        NWS = 4
        wsl = [k.sb("wsl%d" % i, [P, 2048], BF16) for i in range(NWS)]
        pre = [k.sb("pre%d" % i, [P, 3 + GM], BF16) for i in range(2)]
        dg = [k.sb("dg%d" % i, [P, 4, P], BF16) for i in range(2)]
        qkv_h = [k.sb("qkv%d" % i, [P, 12, GM], BF16) for i in range(2)]
        q_h = [t[:, 0:4, :] for t in qkv_h]
        k_h = [t[:, 4:8, :] for t in qkv_h]
        v_h = [t[:, 8:12, :] for t in qkv_h]
        zs_h = [k.sb("zs%d" % i, [P, 4, GM], BF16) for i in range(2)]
        mix_h = [k.sb("mixh%d" % i, [P, 4, GM], BF16) for i in range(2)]
        mix_s = k.sb("mixs", [P, 8, GM], BF16)
        sqs = [k.sb("sqs%d" % i, [P, GM], BF16) for i in range(2)]
        rb = [k.sb("rb%d" % i, [P, GM], F32) for i in range(4)]
        tht = [k.sb("tht%d" % i, [P, GM], F32) for i in range(2)]
        szs = [k.sb("szs%d" % i, [P, GM], BF16) for i in range(2)]
        bzs = [k.sb("bzs%d" % i, [P, GM], BF16) for i in range(2)]
        csb = [k.sb("csb%d" % i, [P, GM], BF16) for i in range(2)]
        preS = [k.sb("preS%d" % i, [P, 2 + GM], BF16) for i in range(2)]
        tailq = [k.sb("tailq%d" % l, [P, 24, 3], BF16) for l in range(NL)]
        tails = [k.sb("tails%d" % l, [P, 8, 2], BF16) for l in range(NL)]
        Sf = [[k.sb("Sf%d_%d" % (l, h), [P, 4, P], F32) for h in range(2)] for l in range(NL)]
        Sb = [[k.sb("Sb%d_%d" % (l, h), [P, 4, P], BF16) for h in range(2)] for l in range(NL)]
        oT_h = [k.sb("oT%d" % i, [P, 4, GM], F32) for i in range(2)]
        xtile = [t.re("p a b -> p (a b)").bitcast(F32)[:, 0:D_MODEL] for t in qkv_h]
        otile = xtile
        tsc = {n: k.sb("tsc_" + n, [P, NTMAX, w], F32) for n, w in
               [("t1", 8), ("e1", 8), ("sp", 8), ("g", 8), ("e2", 8), ("beta", 8), ("nbeta", 8), ("bh", 8),
                ("gcs", 32), ("dif", 8), ("exps", 32), ("ekd", 8), ("kbg", 8)]}
        H = []
        for hf in range(2):
            d = {}
            for n, shp, dt in [("rhsG", [P, 4, P], F32), ("rhsG2", [P, 4, P], F32), ("E", [P, 4, P], F32),
                               ("ET", [P, 4, P], F32), ("nbm", [P, 4, P], F32), ("B0", [P, 4, P], BF16),
                               ("UXa", [P, 4, 2, P], BF16), ("UXb", [P, 4, 2, P], BF16),
                               ("Bka", [P, 4, P], BF16), ("Bkb", [P, 4, P], BF16),
                               ("qkT", [P, 4, P], BF16), ("XT", [P, 4, P], BF16),
                               ("Kbg", [P, 4, P], BF16), ("Kdec", [P, 4, P], BF16), ("Vb", [P, 4, P], BF16),
                               ("u", [P, 4, P], F32), ("wT", [P, 4, P], BF16), ("vnew", [P, 4, P], BF16),
                               ("dgE", [P, 4, P], BF16), ("qd", [P, 4, P], BF16), ("tmpS", [P, 4, P], F32)]:
                d[n] = k.sb("h%d_%s" % (hf, n), shp, dt)
            d["rs"] = d["rhsG"]
            d["t3"] = d["rhsG2"]
            H.append(d)

        fb = [Bank(k.ps("psf%d" % i, [P, 512], F32)) for i in range(8)]
        pools = {"A": [fb[0], fb[1], fb[2], fb[3]], "B0": [fb[4], fb[5]], "B1": [fb[6], fb[7]]}
        pools["b0"] = pools["B0"]
        pools["b1"] = pools["B1"]
        rr = {"A": 0, "B0": 0, "B1": 0}

        def psum(pool):
            bf = pool[0] == "b"
            pool = pool.upper()
            lst = pools[pool]
            b = lst[rr[pool] % len(lst)]
            rr[pool] += 1
            b.gen += 1
            v = V(b.v.ap, b.v.trk, b, b.gen)
            return v.bitcast(BF16) if bf else v

        ident = cst[:, 0, :]
        Ltri = cst[:, 1, :]
        SLm = cst[:, 2, :]
        blk = cst[:, 3, :]
        c0all = cst[:, 4, :]
        c1all = cst[:, 5, :]

        def prm_l(l, off, n):
            return prm[:, l * PL + off: l * PL + off + n]

        def b4(v):
            return v.un(2).bc([P, 4, P])

        def m4(v):
            return v.un(1).bc([P, 4, P])

        def f4(v):
            return v.re("p (a b) -> p a b", a=4)

        def fl(v):
            return v.re("p a b -> p (a b)")

        def rsq_pool(out, ps_in, eps):
            k.act(out, ps_in, AF.Ln, bias=eps)
            k.act(out, out, AF.Exp, scale=-0.5)

        k.dma("sp", prm, params_d)
        k.dma("sp", cst, consts_d)
        k.cp("dve", identb, ident)
        k.memset("pool", ones_mean, 1.0 / 1024.0)
        k.memset("pool", ones_h, 1.0 / 128.0)
        k.memset("pool", ones_q, 128.0)
        k.memset("pool", ones_k, 1.0)
        for l in range(NL):
            k.cp("dve", wabb[:, l, :, :], prm_l(l, 145, 128).re("p (a b) -> p a b", a=8))
            k.act(negA[:, l, :], prm_l(l, 129, 8), AF.Exp)
            k.ts("dve", negA[:, l, :], negA[:, l, :], -1.0, ALU.mult)
            k.ts("dve", ogh[:, l:l + 1], prm_l(l, 128, 1), 0.5, ALU.mult)
            k.ts("dve", scwh[:, l, :], prm_l(l, 104, 24), 0.5, ALU.mult)
            for h in range(2):
                k.memset("pool", Sf[l][h], 0.0)
                k.memset("pool", Sb[l][h], 0.0)
            k.memset("pool", tailq[l], 0.0)
            k.memset("pool", tails[l], 0.0)
        BPC = NBLK // NCH
        for l in range(NL):
            for c in range(NCH):
                k.dma("pool", wbf[l][c], wall[l, c * BPC:(c + 1) * BPC], nsem=4)

        stream = []
        for (t0, nt) in groups:
            for l in range(NL):
                for j in range(NBLK):
                    stream.append((l, j))
        wstate = {"issued": 0, "used": 0}

        def w_prefetch():
            while wstate["issued"] < len(stream) and wstate["issued"] < wstate["used"] + NWS:
                i = wstate["issued"]
                l, j = stream[i]
                k.dma("sp", wsl[i % NWS], wbf[l][j // BPC][j % BPC], nsem=NWS)
                wstate["issued"] += 1

        def w_next():
            w_prefetch()
            i = wstate["used"]
            wstate["used"] += 1
            return wsl[i % NWS]

        cnt = {"ev": 0, "pre": 0, "sq": 0, "sc": 0, "th": 0, "cols": 0}

        def evac_eng():
            cnt["ev"] += 1
            return "act" if cnt["ev"] % 2 else "dve"

        def silu2(dest, ps_in, G):
            th = tht[cnt["th"] % 2]
            cnt["th"] += 1
            k.act(th[:, :G], ps_in, AF.Tanh, scale=0.5)
            k.stt("dve", dest, th[:, :G], 1.0, ps_in, ALU.add, ALU.mult)

        def stream_A(l, G):
            cw = prm_l(l, 8, 96).re("p (c j) -> p c j", j=4)

            def stage1(p_, wblk, sub):
                st = {}
                if p_ < 24:
                    hfx = p_ // 12
                    which = (p_ % 12) // 4
                    h = p_ % 4
                    c = which * 8 + 4 * hfx + h
                    pr = pre[cnt["pre"] % 2]
                    dgs = dg[cnt["pre"] % 2]
                    cnt["pre"] += 1
                    k.cp("pool", pr[:, 0:3], tailq[l][:, c, :])
                    k.tt("pool", dgs, identb.un(1).bc([P, 4, P]), cw[:, c, :].un(2).bc([P, 4, P]), ALU.mult)
                    st.update(hfx=hfx, which=which, h=h, c=c, pr=pr, dgs=dgs)
                elif p_ >= 32 and (p_ - 32) % 4 == 3:
                    m = (p_ - 32) // 4
                    i2 = cnt["sc"] % 2
                    ps_ = preS[i2]
                    dgs = dg[cnt["pre"] % 2]
                    cnt["pre"] += 1
                    k.cp("pool", ps_[:, 0:2], tails[l][:, m, :])
                    k.tt("pool", dgs[:, 0:3, :], identb.un(1).bc([P, 3, P]),
                         scwh[:, l, 3 * m:3 * m + 3].un(2).bc([P, 3, P]), ALU.mult)
                    st.update(ps_=ps_, dgs=dgs)
                pp = psum("A")
                for kk in range(8):
                    k.mm(pp[:, :G], wblk[:, kk, sub * P:(sub + 1) * P], hT[:, kk, :G],
                         start=(kk == 0), stop=(kk == 7))
                st["pp"] = pp
                return st

            def stage2(p_, st):
                pp = st["pp"]
                if p_ < 24:
                    hfx, which, h, c, pr, dgs = st["hfx"], st["which"], st["h"], st["c"], st["pr"], st["dgs"]
                    k.cp(evac_eng(), pr[:, 3:3 + G], pp[:, :G])
                    k.cp("pool", tailq[l][:, c, :], pr[:, G:G + 3])
                    cp_ = psum("A")
                    for jj in range(4):
                        k.mm(cp_[:, :G], dgs[:, jj, :], pr[:, jj:jj + G], start=(jj == 0), stop=(jj == 3))
                    dest = qkv_h[hfx][:, which * 4 + h, :G]
                    silu2(dest, cp_[:, :G], G)
                    if which < 2:
                        s2 = sqs[cnt["sq"] % 2]
                        cnt["sq"] += 1
                        rbb = rb[h]
                        k.tt("pool", s2[:, :G], dest, dest, ALU.mult)
                        np_ = psum("A")
                        k.mm(np_[:, :G], ones_q if which == 0 else ones_k, s2[:, :G])
                        k.cp("dve", rbb[:, :G], np_[:, :G])
                        if h == 3:
                            eps_ = (512.0 * EPS) if which == 0 else 4.0 * EPS
                            for h_ in range(4):
                                k.act(rb[h_][:, :G], rb[h_][:, :G], AF.Ln, bias=eps_)
                            for h_ in range(4):
                                k.act(rb[h_][:, :G], rb[h_][:, :G], AF.Exp, scale=-0.5)
                            for h_ in range(4):
                                dd = qkv_h[hfx][:, which * 4 + h_, :G]
                                k.tt("dve", dd, dd, rb[h_][:, :G], ALU.mult)
                elif p_ < 32:
                    hh = p_ - 24
                    silu2(zs_h[hh // 4][:, hh % 4, :G], pp[:, :G], G)
                else:
                    m = (p_ - 32) // 4
                    r = (p_ - 32) % 4
                    i2 = cnt["sc"] % 2
                    if r == 0:
                        silu2(szs[i2][:, :G], pp[:, :G], G)
                    elif r == 1:
                        k.tt("dve", bzs[i2][:, :G], pp[:, :G], szs[i2][:, :G], ALU.mult)
                    elif r == 2:
                        k.cp("act", csb[i2][:, :G], pp[:, :G])
                    else:
                        ps_, dgs = st["ps_"], st["dgs"]
                        k.tt("dve", ps_[:, 2:2 + G], pp[:, :G], csb[i2][:, :G], ALU.mult)
                        k.cp("pool", tails[l][:, m, :], ps_[:, G:G + 2])
                        cv = psum("A")
                        for jj in range(3):
                            k.mm(cv[:, :G], dgs[:, jj, :], ps_[:, jj:jj + G], start=(jj == 0), stop=(jj == 2))
                        k.tt("dve", mix_s[:, m, :G], cv[:, :G], bzs[i2][:, :G], ALU.mult)
                        cnt["sc"] += 1
                cnt["cols"] += 1

            pending = None
            for j in range(32):
                wblk = w_next().re("p (k c) -> p k c", k=8)
                for sub in range(2):
                    p_ = 2 * j + sub
                    st = stage1(p_, wblk, sub)
                    if pending is not None:
                        stage2(*pending)
                    pending = (p_, st)
                    yield
            stage2(*pending)
            yield

        def stream_B(l, hf, nt):
            d = H[hf]
            pf = "B%d" % hf
            pb16 = "b%d" % hf
            qh, kh, vh = q_h[hf], k_h[hf], v_h[hf]
            hs = slice(4 * hf, 4 * hf + 4)
            S_f, S_b = Sf[l][hf], Sb[l][hf]
            oT = oT_h[hf]
            for tau in range(nt):
                tc = slice(tau * P, (tau + 1) * P)

                def sc(name, lo=0):
                    return tsc[name][:, tau, lo + 4 * hf: lo + 4 * hf + 4]
                pk = psum(pb16)
                for h in range(4):
                    k.tr(pk[:, h * P:(h + 1) * P], kh[:, h, tc], identb)
                pkv = f4(pk[:, 0:512])
                k.tt("dve", d["Kbg"], pkv, b4(sc("kbg")), ALU.mult)
                k.tt("dve", d["Kdec"], pkv, b4(sc("ekd")), ALU.mult)
                yield
                pv_ = psum(pb16)
                for h in range(4):
                    k.tr(pv_[:, h * P:(h + 1) * P], vh[:, h, tc], identb)
                k.tt("dve", d["Vb"], f4(pv_[:, 0:512]), b4(sc("bh")), ALU.mult)
                yield
                k.tt("pool", d["rhsG"], m4(SLm), b4(sc("g")), ALU.mult)
                pD = psum(pf)
                k.mm(pD, Ltri, fl(d["rhsG"]))
                k.act(fl(d["E"]), pD, AF.Exp)
                yield
                k.tt("pool", d["rhsG2"], m4(Ltri), b4(sc("g")), ALU.mult)
                pDT = psum(pf)
                k.mm(pDT, SLm, fl(d["rhsG2"]))
                k.act(fl(d["ET"]), pDT, AF.Exp)
                yield
                k.tt("pool", d["nbm"], m4(SLm), b4(sc("nbeta")), ALU.mult)
                k.tt("pool", d["E"], d["E"], d["nbm"], ALU.mult)
                pKK = psum(pf)
                for h in range(4):
                    k.mm(pKK[:, h * P:(h + 1) * P], kh[:, h, tc], kh[:, h, tc])
                k.tt("dve", fl(d["B0"]), pKK, fl(d["E"]), ALU.mult)
                yield
                k.tt("pool", d["ET"], d["ET"], m4(Ltri), ALU.mult)
                pQK = psum(pf)
                for h in range(4):
                    k.mm(pQK[:, h * P:(h + 1) * P], kh[:, h, tc], qh[:, h, tc])
                k.tt("dve", fl(d["qkT"]), pQK, fl(d["ET"]), ALU.mult)
                yield
                k.tt("pool", d["dgE"], m4(identb), b4(sc("exps")), ALU.mult)
                pe_ = psum(pf)
                k.mm(pe_, ones_k, fl(d["dgE"]))
                k.tt("dve", d["qd"], qh[:, :, tc], f4(pe_), ALU.mult)
                yield
                pU = psum(pb16)
                for h in range(4):
                    k.tr(pU[:, h * P:(h + 1) * P], d["B0"][:, h, :], identb)
                UX, UXn = d["UXa"], d["UXb"]
                Bk, Bkn = d["Bka"], d["Bkb"]
                k.cp("act", UX[:, :, 0, :], f4(pU[:, 0:512]))
                yield
                p1 = psum(pf)
                for h in range(4):
                    k.mm(p1[:, h * P:(h + 1) * P], d["B0"][:, h, :], UX[:, h, 0, :])
                k.cp("act", UXn[:, :, 0, :], f4(p1))
                k.tt("dve", UXn[:, :, 1, :], UX[:, :, 0, :], m4(identb), ALU.add)
                yield
                p2 = psum(pf)
                for h in range(4):
                    k.mm(p2[:, h * P:(h + 1) * P], UX[:, h, 0, :], d["B0"][:, h, :])
                k.cp("act", Bkn, f4(p2))
                UX, UXn = UXn, UX
                Bk, Bkn = Bkn, Bk
                yield
                for lev in range(1, 4):
                    for hp in range(2):
                        pa = psum(pf)
                        for h2 in range(2):
                            h = 2 * hp + h2
                            k.mm(pa[:, h2 * 256:h2 * 256 + 256], Bk[:, h, :], UX[:, h, :, :].re("p a b -> p (a b)"))
                        pv = pa.re("p (a t b) -> p a t b", a=2, t=2)
                        k.cp("act", UXn[:, 2 * hp:2 * hp + 2, 0, :], pv[:, :, 0, :])
                        k.tt("dve", UXn[:, 2 * hp:2 * hp + 2, 1, :], UX[:, 2 * hp:2 * hp + 2, 1, :],
                             pv[:, :, 1, :], ALU.add)
                        yield
                    pb_ = psum(pf)
                    for h in range(4):
                        k.mm(pb_[:, h * P:(h + 1) * P], UX[:, h, 0, :], Bk[:, h, :])
                    k.cp("act", Bkn, f4(pb_))
                    UX, UXn = UXn, UX
                    Bk, Bkn = Bkn, Bk
                    yield
                px = psum(pf)
                for h in range(4):
                    k.mm(px[:, h * P:(h + 1) * P], Bk[:, h, :], UX[:, h, 1, :])
                k.tt("dve", UXn[:, :, 1, :], UX[:, :, 1, :], f4(px), ALU.add)
                yield
                pb_ = psum(pf)
                for h in range(4):
                    k.mm(pb_[:, h * P:(h + 1) * P], UX[:, h, 0, :], Bk[:, h, :])
                k.cp("act", Bkn, f4(pb_))
                UX, UXn = UXn, UX
                Bk, Bkn = Bkn, Bk
                yield
                px = psum(pf)
                for h in range(4):
                    k.mm(px[:, h * P:(h + 1) * P], Bk[:, h, :], UX[:, h, 1, :])
                k.tt("dve", d["XT"], UX[:, :, 1, :], f4(px), ALU.add)
                yield
                pu = psum(pf)
                for h in range(4):
                    k.mm(pu[:, h * P:(h + 1) * P], d["XT"][:, h, :], d["Vb"][:, h, :])
                k.cp("act", d["u"], f4(pu))
                yield
                pw = psum(pf)
                for h in range(4):
                    k.mm(pw[:, h * P:(h + 1) * P], d["Kbg"][:, h, :], d["XT"][:, h, :])
                k.cp("act", d["wT"], f4(pw))
                yield
                for c in range(2):
                    cs = slice(64 * c, 64 * c + 64)
                    pws = psum(pf)
                    for h in range(4):
                        k.mm(pws[cs, h * P:(h + 1) * P], d["wT"][:, h, cs], S_b[:, h, :])
                    k.tt("dve", d["vnew"][cs, :, :], d["u"][cs, :, :], f4(pws[cs, :]), ALU.subtract)
                    yield
                    po = psum(pf)
                    for h in range(4):
                        k.mm(po[:, h * 64:(h + 1) * 64], S_b[:, h, :], d["qd"][:, h, cs], start=True, stop=False)
                        k.mm(po[:, h * 64:(h + 1) * 64], d["vnew"][cs, h, :], d["qkT"][cs, h, cs],
                             start=False, stop=True)
                    k.cp("act", oT[:, :, tau * P + 64 * c: tau * P + 64 * c + 64],
                         po[:, 0:256].re("p (a b) -> p a b", a=4))
                    yield
                    pdS = psum(pf)
                    for h in range(4):
                        k.mm(pdS[:, h * P:(h + 1) * P], d["Kdec"][cs, h, :], d["vnew"][cs, h, :])
                    k.tt("pool", d["tmpS"], S_f, b4(sc("exps", 16 + 8 * c)), ALU.mult)
                    k.tt("dve", S_f, d["tmpS"], f4(pdS), ALU.add)
                    k.cp("act", S_b, S_f)
                    yield

        xt_i = 0
        ot_i = 0
        for (t0, nt) in groups:
            G = nt * P
            for tau in range(nt):
                xt = xtile[xt_i % 2]
                xt_i += 1
                k.dma("sp", xt, xin[(t0 + tau) * P:(t0 + tau + 1) * P, :])
                for half in range(2):
                    pt = psum("A")
                    for f in range(4):
                        ff = half * 4 + f
                        k.tr(pt[:, f * P:(f + 1) * P], xt[:, ff * P:(ff + 1) * P], ident)
                    k.cp(evac_eng(), xT[:, half * 4:half * 4 + 4, tau * P:(tau + 1) * P], f4(pt))
            for l in range(NL):
                g_l = prm_l(l, 0, 8)
                dtb = prm_l(l, 137, 8)
                sq = mix_s[:, :, :G]
                k.act(sq, xT[:, :, :G], AF.Square)
                pn = psum("A")
                for kk in range(8):
                    k.mm(pn[:, :G], ones_mean, sq[:, kk, :], start=(kk == 0), stop=(kk == 7))
                rsq_pool(rstd[:, :G], pn[:, :G], EPS)
                for kk in range(8):
                    k.stt("dve", hT[:, kk, :G], xT[:, kk, :G], g_l[:, kk:kk + 1], rstd[:, :G], ALU.mult, ALU.mult)
                pab = psum("A")
                for tau in range(nt):
                    for kk in range(8):
                        k.mm(pab[:, tau * 16:(tau + 1) * 16], hT[:, kk, tau * P:(tau + 1) * P], wabb[:, l, kk, :],
                             start=(kk == 0), stop=(kk == 7))
                pabv = pab[:, 0:nt * 16].re("p (t w) -> p t w", w=16)

                def T_(n):
                    return tsc[n][:, :nt, :]
                k.tt("dve", T_("t1"), pabv[:, :, 0:8], dtb.un(1).bc([P, nt, 8]), ALU.add)
                k.act(T_("e2"), pabv[:, :, 8:16], AF.Exp, scale=-1.0)
                k.act(T_("e1"), T_("t1"), AF.Exp)
                k.act(T_("sp"), T_("e1"), AF.Ln, bias=1.0)
                k.tt("dve", T_("g"), T_("sp"), negA[:, l, :].un(1).bc([P, nt, 8]), ALU.mult)
                k.ts("dve", T_("e2"), T_("e2"), 1.0, ALU.add)
                k.recip(T_("beta"), T_("e2"))
                k.ts("dve", T_("nbeta"), T_("beta"), -1.0, ALU.mult)
                k.ts("dve", T_("bh"), T_("beta"), 0.5, ALU.mult)
                pcs = psum("A")
                for tau in range(nt):
                    for ii, mat in enumerate((Ltri, blk, c0all, c1all)):
                        k.mm(pcs[:, tau * 32 + ii * 8: tau * 32 + ii * 8 + 8], mat, tsc["g"][:, tau, :])
                k.cp("dve", T_("gcs"), pcs[:, 0:nt * 32].re("p (t w) -> p t w", w=32))
                k.tt("dve", T_("dif"), T_("gcs")[:, :, 8:16], T_("gcs")[:, :, 0:8], ALU.subtract)
                k.act(T_("exps"), T_("gcs"), AF.Exp)
                k.act(T_("ekd"), T_("dif"), AF.Exp)
                k.tt("dve", T_("kbg"), T_("beta"), T_("exps")[:, :, 0:8], ALU.mult)
                cnt["cols"] = 0
                A = stream_A(l, G)
                Bs = [stream_B(l, 0, nt), stream_B(l, 1, nt)]
                started = [False, False]
                active = [A]
                while active:
                    for g_ in list(active):
                        try:
                            next(g_)
                        except StopIteration:
                            active.remove(g_)
                    if not started[0] and cnt["cols"] >= 12:
                        active.append(Bs[0])
                        started[0] = True
                    if not started[1] and cnt["cols"] >= 24:
                        active.append(Bs[1])
                        started[1] = True
                for hf in range(2):
                    osq = qkv_h[hf][:, 0:4, :G]
                    k.act(osq, oT_h[hf][:, :, :G], AF.Square)
                    pns = []
                    for h in range(4):
                        pn2 = psum("B%d" % hf)
                        k.mm(pn2[:, :G], ones_h, osq[:, h, :])
                        k.cp("dve", rb[h][:, :G], pn2[:, :G])
                    for h in range(4):
                        k.act(rb[h][:, :G], rb[h][:, :G], AF.Ln, bias=EPS)
                    for h in range(4):
                        k.act(rb[h][:, :G], rb[h][:, :G], AF.Exp, scale=-0.5)
                    for h in range(4):
                        k.stt("dve", oT_h[hf][:, h, :G], oT_h[hf][:, h, :G], ogh[:, l:l + 1], rb[h][:, :G],
                              ALU.mult, ALU.mult)
                        k.tt("pool", mix_h[hf][:, h, :G], oT_h[hf][:, h, :G], zs_h[hf][:, h, :G], ALU.mult)
                for f in range(8):
                    wblk = w_next().re("p (m c) -> p m c", m=16)
                    po = psum("A")
                    for m in range(16):
                        rhs = mix_h[0][:, m, :G] if m < 4 else (mix_h[1][:, m - 4, :G] if m < 8 else mix_s[:, m - 8, :G])
                        k.mm(po[:, :G], wblk[:, m, :], rhs, start=(m == 0), stop=(m == 15))
                    k.tt("dve", xT[:, f, :G], xT[:, f, :G], po[:, :G], ALU.add)
            if final:
                fg = prm[:, NL * PL: NL * PL + 8]
                sq = mix_s[:, :, :G]
                k.act(sq, xT[:, :, :G], AF.Square)
                pn = psum("A")
                for kk in range(8):
                    k.mm(pn[:, :G], ones_mean, sq[:, kk, :], start=(kk == 0), stop=(kk == 7))
                rsq_pool(rstd[:, :G], pn[:, :G], EPS)
                for kk in range(8):
                    k.stt("dve", xT[:, kk, :G], xT[:, kk, :G], fg[:, kk:kk + 1], rstd[:, :G], ALU.mult, ALU.mult)
            for tau in range(nt):
                gt = t0 + tau
                if final and gt == 0:
                    continue
                ot = otile[ot_i % 2]
                ot_i += 1
                for half in range(2):
                    pt = psum("A")
                    for f in range(4):
                        ff = half * 4 + f
                        k.tr(pt[:, f * P:(f + 1) * P], xT[:, ff, tau * P:(tau + 1) * P], ident)
                    k.cp(evac_eng(), ot[:, half * 512:(half + 1) * 512], pt)
                orow = (gt - 1) if final else gt
                k.dma("sp", out_d[orow * P:(orow + 1) * P, :], ot)
        k.finish([out_d])
    return nc


def _consts():
    i = np.arange(P)
    same = (i[:, None] // 64) == (i[None, :] // 64)
    c = np.zeros((P, 6, P), np.float32)
    c[:, 0, :] = np.eye(P)
    c[:, 1, :] = same & (i[:, None] <= i[None, :])
    c[:, 2, :] = same & (i[:, None] > i[None, :])
    c[:, 3, :] = same
    c[:, 4, :] = (i[:, None] < 64)
    c[:, 5, :] = (i[:, None] >= 64)
    return c


def _layout_weights(w_in, w_out, layers):
    NL = len(layers)
    wall = np.empty((NL, NBLK, P, 2048), np.float32)
    wab = np.empty((NL, P, 128), np.float32)
    DN = 1024
    for n, l in enumerate(layers):
        W = w_in[l]
        qkv = W[:, 0:3 * DN]
        z = W[:, 3 * DN:4 * DN]
        ab = W[:, 4 * DN:4 * DN + 16]
        o = 4 * DN + 16
        scb = W[:, o:o + 1024]
        scc = W[:, o + 1024:o + 2048]
        sch = W[:, o + 2048:o + 3072]
        scz = W[:, o + 3072:o + 4096]
        sc = np.stack([scz.reshape(1024, 8, 128), scb.reshape(1024, 8, 128), scc.reshape(1024, 8, 128),
                       sch.reshape(1024, 8, 128)], axis=2).reshape(1024, 4096)
        q_, k_, v_ = qkv[:, 0:DN], qkv[:, DN:2 * DN], qkv[:, 2 * DN:3 * DN]
        dn = np.concatenate([q_[:, 0:512], k_[:, 0:512], v_[:, 0:512], q_[:, 512:], k_[:, 512:], v_[:, 512:]], axis=1)
        main = np.concatenate([dn, z, sc], axis=1)
        wall[n, 0:32] = main.reshape(8, P, 32, 256).transpose(2, 1, 0, 3).reshape(32, P, 2048)
        wall[n, 32:40] = w_out[l].reshape(16, P, 8, 128).transpose(2, 1, 0, 3).reshape(8, P, 2048)
        wab[n] = ab.reshape(8, P, 16).transpose(1, 0, 2).reshape(P, 128)
    return wall, wab


def _layout_params(norm_g, dn_conv_w, dn_A_log, dn_dt_bias, dn_out_g, sc_conv_w, final_g, wab, layers):
    NL = len(layers)
    prm = np.zeros((P, NL * PL + 8), np.float32)
    for n, l in enumerate(layers):
        b = n * PL
        prm[:, b:b + 8] = norm_g[l].reshape(8, P).T
        prm[:, b + 8:b + 104] = dn_conv_w[l].reshape(4, 24, P).transpose(2, 1, 0).reshape(P, 96)
        prm[:, b + 104:b + 128] = sc_conv_w[l].reshape(3, 8, P).transpose(2, 1, 0).reshape(P, 24)
        prm[:, b + 128] = dn_out_g[l]
        prm[:, b + 129:b + 137] = np.broadcast_to(dn_A_log[l][None, :], (P, 8))
        prm[:, b + 137:b + 145] = np.broadcast_to(dn_dt_bias[l][None, :], (P, 8))
        prm[:, b + 145:b + 273] = wab[n]
    prm[:, NL * PL:NL * PL + 8] = final_g.reshape(8, P).T
    return prm


_CACHE = {}


def _get_program(T, groups, NL, first, final):
    key = (T, tuple(groups), NL, first, final)
    if key not in _CACHE:
        _CACHE[key] = build_program(T, groups, NL, first, final)
    return _CACHE[key]


def make_groups(T, nt):
    groups = []
    t = 0
    while t < T:
        n = min(nt, T - t)
        groups.append((t, n))
        t += n
    return groups


FUSED = True
GROUP_NT = 3


def kernel(x, meta_tokens, norm_g, w_in, dn_conv_w, dn_A_log, dn_dt_bias, dn_out_g, sc_conv_w, w_out, final_g):
    x = np.asarray(x, np.float32)
    B, SEQ, D = x.shape
    T = (SEQ + P) // P
    assert (T - 1) * P == SEQ
    xin = np.zeros((B, T * P, D), np.float32)
    xin[:, P - N_META:P, :] = np.asarray(meta_tokens, np.float32)[None]
    xin[:, P:, :] = x
    consts = _consts()
    groups = make_groups(T, GROUP_NT)
    args = [np.asarray(a, np.float32) for a in (norm_g, dn_conv_w, dn_A_log, dn_dt_bias, dn_out_g, sc_conv_w, final_g)]
    w_in = np.asarray(w_in, np.float32)
    w_out = np.asarray(w_out, np.float32)
    plan = [list(range(DEPTH))] if FUSED else [[l] for l in range(DEPTH)]
    cur = xin
    for pi, layers in enumerate(plan):
        first = pi == 0
        final = pi == len(plan) - 1
        wall, wab = _layout_weights(w_in, w_out, layers)
        prm = _layout_params(*args, wab, layers)
        nc = _get_program(T, groups, len(layers), first, final)
        in_maps = [{"xin": np.ascontiguousarray(cur[b]), "wall": wall, "params": prm, "consts": consts}
                   for b in range(B)]
        res = run_bass_kernel_spmd(nc, in_maps, core_ids=list(range(B)))
        cur = np.stack([np.asarray(r["out"]) for r in res.results], axis=0)
    return cur.astype(np.float32)
```

```python
import numpy as np
from contextlib import ExitStack
import concourse.bass as bass
import concourse.mybir as mybir
from concourse.bass_utils import run_bass_kernel_spmd

F32 = mybir.dt.float32
BF16 = mybir.dt.bfloat16
ALU = mybir.AluOpType
AF = mybir.ActivationFunctionType

D_MODEL = 1024
N_META = 16
DEPTH = 4
EPS = 1e-6
P = 128
NBLK = 40
PL = 8 + 96 + 24 + 1 + 8 + 8 + 128
SEM_LIMIT = 4000


class Trk:
    __slots__ = ("w", "r")

    def __init__(self):
        self.w = {}
        self.r = {}


class V:
    def __init__(self, ap, trk, bank=None, gen=None):
        self.ap = ap
        self.trk = trk
        self.bank = bank
        self.gen = gen

    def _n(self, ap):
        return V(ap, self.trk, self.bank, self.gen)

    def __getitem__(self, k):
        return self._n(self.ap[k])

    def re(self, pat, **kw):
        return self._n(self.ap.rearrange(pat, **kw))

    def bc(self, shape):
        return self._n(self.ap.to_broadcast(list(shape)))

    def un(self, axis):
        return self._n(self.ap.unsqueeze(axis))

    def bitcast(self, dt):
        return self._n(self.ap.bitcast(dt))


class Bank:
    def __init__(self, v):
        self.v = v
        self.gen = 0


class KB:
    ENGS = ("pe", "act", "dve", "pool", "sp")

    def __init__(self, nc, es):
        self.nc = nc
        self.es = es
        self.q = {e: [] for e in self.ENGS}
        self.waited = {e: {} for e in self.ENGS}
        self.dma_ring = {}
        self.n_dma_sems = 0

    def sb(self, name, shape, dt):
        t = self.es.enter_context(self.nc.sbuf_tensor(name, list(shape), dt))
        return V(t[:], Trk())

    def ps(self, name, shape, dt):
        t = self.es.enter_context(self.nc.psum_tensor(name, list(shape), dt))
        return V(t[:], Trk())

    def dram(self, name, shape, dt, kind):
        t = self.nc.dram_tensor(name, list(shape), dt, kind=kind)
        return V(t.ap(), Trk())

    def _emit(self, eng, fn, reads, writes, dma=None):
        idx = len(self.q[eng])
        deps = {}

        def add(key, ev):
            old = deps.get(key)
            if old is None or ev[1] > old[1]:
                deps[key] = ev

        for v in reads:
            if v.bank is not None and v.bank.gen != v.gen:
                raise RuntimeError("stale PSUM view read")
            for key, ev in v.trk.w.items():
                add(key, ev)
            if v.bank is not None:
                for key, ev in v.trk.r.items():
                    if key != eng:
                        add(key, ev)
        for v in writes:
            if v.bank is not None and v.bank.gen != v.gen:
                raise RuntimeError("stale PSUM view write")
            for key, ev in v.trk.w.items():
                if key != eng or eng != "pe":
                    add(key, ev)
            for key, ev in v.trk.r.items():
                if key != eng or eng != "pe":
                    add(key, ev)
        waits = []
        wd = self.waited[eng]
        for key, ev in deps.items():
            if wd.get(key, -1) >= ev[1]:
                continue
            wd[key] = ev[1]
            waits.append(ev)
            if ev[0] == "cc":
                self.q[ev[2]][ev[1]]["sig"] = True
        item = {"fn": fn, "waits": waits, "sig": False}
        self.q[eng].append(item)
        if dma is None:
            key, ev = eng, ("cc", idx, eng)
        else:
            key, ev = dma
        for v in reads:
            v.trk.r[key] = ev
        for v in writes:
            v.trk.w[key] = ev
        return item

    def mm(self, out, lhsT, rhs, start=True, stop=True, **kw):
        def fn(e):
            return e.matmul(out.ap, lhsT.ap, rhs.ap, start=start, stop=stop, **kw)
        self._emit("pe", fn, [lhsT, rhs], [out])

    def tr(self, out, in_, ident):
        def fn(e):
            return e.transpose(out.ap, in_.ap, ident.ap)
        self._emit("pe", fn, [in_, ident], [out])

    def act(self, out, in_, func, bias=None, scale=1.0, eng="act"):
        rds = [in_]
        kw = {}
        if bias is not None:
            if isinstance(bias, V):
                rds.append(bias)
                kw["bias"] = bias.ap
            else:
                kw["bias"] = float(bias)
        if isinstance(scale, V):
            rds.append(scale)
            sc = scale.ap
        else:
            sc = float(scale)

        def fn(e):
            return e.activation(out.ap, in_.ap, func, scale=sc, **kw)
        self._emit(eng, fn, rds, [out])

    def tt(self, eng, out, a, b, op):
        def fn(e):
            return e.tensor_tensor(out.ap, a.ap, b.ap, op)
        self._emit(eng, fn, [a, b], [out])

    def ts(self, eng, out, a, s1, op0, s2=None, op1=None):
        rds = [a]
        s1a = s1
        if isinstance(s1, V):
            rds.append(s1)
            s1a = s1.ap
        s2a = s2
        if isinstance(s2, V):
            rds.append(s2)
            s2a = s2.ap
        if op1 is None:
            def fn(e):
                return e.tensor_single_scalar(out.ap, a.ap, s1a, op0)
        else:
            def fn(e):
                return e.tensor_scalar(out.ap, a.ap, s1a, s2a, op0, op1)
        self._emit(eng, fn, rds, [out])

    def stt(self, eng, out, a, sc, b, op0, op1):
        rds = [a, b]
        sca = sc
        if isinstance(sc, V):
            rds.append(sc)
            sca = sc.ap

        def fn(e):
            return e.scalar_tensor_tensor(out.ap, a.ap, sca, b.ap, op0, op1)
        self._emit(eng, fn, rds, [out])

    def cp(self, eng, out, a):
        if eng == "act":
            return self.act(out, a, AF.Copy)

        def fn(e):
            return e.tensor_copy(out.ap, a.ap)
        self._emit(eng, fn, [a], [out])

    def memset(self, eng, out, val):
        def fn(e):
            return e.memset(out.ap, val)
        self._emit(eng, fn, [], [out])

    def rsq(self, out, in_, eps):
        self.act(out, in_, AF.Ln, bias=eps)
        self.act(out, out, AF.Exp, scale=-0.5)

    def recip(self, out, a):
        def fn(e):
            return e.reciprocal(out.ap, a.ap)
        self._emit("dve", fn, [a], [out])

    def dma(self, q, out, in_, nsem=8):
        ring = self.dma_ring.setdefault(q, {"sems": [], "pos": 0})
        if len(ring["sems"]) < nsem:
            s = self.es.enter_context(self.nc.semaphore("dsem%d" % self.n_dma_sems))
            self.n_dma_sems += 1
            ring["sems"].append([s, 0, "d%d" % self.n_dma_sems])
        slot = ring["sems"][ring["pos"] % len(ring["sems"])]
        ring["pos"] += 1
        sem, cnt, key = slot
        slot[1] = cnt + 16

        def fn(e):
            return e.dma_start(out=out.ap, in_=in_.ap).then_inc(sem, 16)
        item = self._emit(q, fn, [in_], [out], dma=(key, ("d", cnt + 16, sem)))
        if cnt > 0 and self.waited[q].get(key, -1) < cnt:
            item["waits"].append(("d", cnt, sem))
            self.waited[q][key] = cnt
        return item

    def finish(self, final_waits):
        nc = self.nc
        self._emit("sp", None, final_waits, [])
        sems = {}
        for e in self.ENGS:
            n = 0
            for it in self.q[e]:
                if it["sig"]:
                    it["signo"] = n
                    n += 1
            nsem = max(1, (n + SEM_LIMIT - 1) // SEM_LIMIT)
            sems[e] = [self.es.enter_context(nc.semaphore("s_%s_%d" % (e, i))) for i in range(nsem)]

        def semval(e, signo):
            return sems[e][signo // SEM_LIMIT], (signo % SEM_LIMIT) + 1

        def run(ename, eng):
            for it in self.q[ename]:
                for ev in it["waits"]:
                    if ev[0] == "c":
                        raise RuntimeError("unreachable")
                    elif ev[0] == "cc":
                        s, v = semval(ev[2], self.q[ev[2]][ev[1]]["signo"])
                        eng.wait_ge(s, v)
                    else:
                        eng.wait_ge(ev[2], ev[1])
                if it["fn"] is None:
                    continue
                ins = it["fn"](eng)
                if it["sig"]:
                    s, v = semval(ename, it["signo"])
                    ins.then_inc(s, 1)

        with nc.Block() as block:
            @block.tensor
            def _(e):
                run("pe", e)

            @block.scalar
            def _(e):
                run("act", e)

            @block.vector
            def _(e):
                run("dve", e)

            @block.gpsimd
            def _(e):
                run("pool", e)

            @block.sync
            def _(e):
                run("sp", e)


def build_program(T, groups, NL, first, final):
    nc = bass.Bass("TRN2", target_bir_lowering=False)
    NTMAX = max(nt for _, nt in groups)
    GM = NTMAX * P
    NP_ = NL * PL + 8
    with ExitStack() as es:
        k = KB(nc, es)
        xin = k.dram("xin", [T * P, D_MODEL], F32, "ExternalInput")
        wall = k.dram("wall", [NL, NBLK, P, 2048], F32, "ExternalInput")
        params_d = k.dram("params", [P, NP_], F32, "ExternalInput")
        consts_d = k.dram("consts", [P, 6, P], F32, "ExternalInput")
        n_out_tiles = T - 1 if final else T
        out_d = k.dram("out", [n_out_tiles * P, D_MODEL], F32, "ExternalOutput")
        NCH = 5
        wbf = [[k.dram("wbf_%d_%d" % (l, c), [NBLK // NCH, P, 2048], BF16, "Internal") for c in range(NCH)]
               for l in range(NL)]

        prm = k.sb("prm", [P, NP_], F32)
        cst = k.sb("cst", [P, 6, P], F32)
        identb = k.sb("identb", [P, P], BF16)
        ones_mean = k.sb("ones_mean", [P, P], BF16)
        ones_h = k.sb("ones_h", [P, P], BF16)
        ones_q = k.sb("ones_q", [P, P], BF16)
        ones_k = k.sb("ones_k", [P, P], BF16)
        wabb = k.sb("wabb", [P, NL, 8, 16], BF16)
        negA = k.sb("negA", [P, NL, 8], F32)
        ogh = k.sb("ogh", [P, NL], F32)
        scwh = k.sb("scwh", [P, NL, 24], F32)
        xT = k.sb("xT", [P, 8, GM], F32)
        hT = k.sb("hT", [P, 8, GM], BF16)
        rstd = k.sb("rstd", [P, GM], F32)
        NWS = 4
        wsl = [k.sb("wsl%d" % i, [P, 2048], BF16) for i in range(NWS)]
        pre = [k.sb("pre%d" % i, [P, 3 + GM], BF16) for i in range(2)]
        dg = [k.sb("dg%d" % i, [P, 4, P], BF16) for i in range(2)]
        qkv_h = [k.sb("qkv%d" % i, [P, 12, GM], BF16) for i in range(2)]
        q_h = [t[:, 0:4, :] for t in qkv_h]
        k_h = [t[:, 4:8, :] for t in qkv_h]
        v_h = [t[:, 8:12, :] for t in qkv_h]
        zs_h = [k.sb("zs%d" % i, [P, 4, GM], BF16) for i in range(2)]
        mix_h = [k.sb("mixh%d" % i, [P, 4, GM], BF16) for i in range(2)]
        mix_s = k.sb("mixs", [P, 8, GM], BF16)
        sqs = [k.sb("sqs%d" % i, [P, GM], BF16) for i in range(2)]
        rb = [k.sb("rb%d" % i, [P, GM], F32) for i in range(4)]
        tht = [k.sb("tht%d" % i, [P, GM], F32) for i in range(2)]
        szs = [k.sb("szs%d" % i, [P, GM], BF16) for i in range(2)]
        bzs = [k.sb("bzs%d" % i, [P, GM], BF16) for i in range(2)]
        csb = [k.sb("csb%d" % i, [P, GM], BF16) for i in range(2)]
        preS = [k.sb("preS%d" % i, [P, 2 + GM], BF16) for i in range(2)]
        tailq = [k.sb("tailq%d" % l, [P, 24, 3], BF16) for l in range(NL)]
        tails = [k.sb("tails%d" % l, [P, 8, 2], BF16) for l in range(NL)]
        Sf = [[k.sb("Sf%d_%d" % (l, h), [P, 4, P], F32) for h in range(2)] for l in range(NL)]
        Sb = [[k.sb("Sb%d_%d" % (l, h), [P, 4, P], BF16) for h in range(2)] for l in range(NL)]
        oT_h = [k.sb("oT%d" % i, [P, 4, GM], F32) for i in range(2)]
        xtile = [t.re("p a b -> p (a b)").bitcast(F32)[:, 0:D_MODEL] for t in qkv_h]
        otile = xtile
        tsc = {n: k.sb("tsc_" + n, [P, NTMAX, w], F32) for n, w in
               [("t1", 8), ("e1", 8), ("sp", 8), ("g", 8), ("e2", 8), ("beta", 8), ("nbeta", 8), ("bh", 8),
                ("gcs", 32), ("dif", 8), ("exps", 32), ("ekd", 8), ("kbg", 8)]}
        H = []
        for hf in range(2):
            d = {}
            for n, shp, dt in [("rhsG", [P, 4, P], F32), ("rhsG2", [P, 4, P], F32), ("E", [P, 4, P], F32),
                               ("ET", [P, 4, P], F32), ("nbm", [P, 4, P], F32), ("B0", [P, 4, P], BF16),
                               ("UXa", [P, 4, 2, P], BF16), ("UXb", [P, 4, 2, P], BF16),
                               ("Bka", [P, 4, P], BF16), ("Bkb", [P, 4, P], BF16),
                               ("qkT", [P, 4, P], BF16), ("XT", [P, 4, P], BF16),
                               ("Kbg", [P, 4, P], BF16), ("Kdec", [P, 4, P], BF16), ("Vb", [P, 4, P], BF16),
                               ("u", [P, 4, P], F32), ("wT", [P, 4, P], BF16), ("vnew", [P, 4, P], BF16),
                               ("dgE", [P, 4, P], BF16), ("qd", [P, 4, P], BF16), ("tmpS", [P, 4, P], F32)]:
                d[n] = k.sb("h%d_%s" % (hf, n), shp, dt)
            d["rs"] = d["rhsG"]
            d["t3"] = d["rhsG2"]
            H.append(d)

        fb = [Bank(k.ps("psf%d" % i, [P, 512], F32)) for i in range(8)]
        pools = {"A": [fb[0], fb[1], fb[2], fb[3]], "B0": [fb[4], fb[5]], "B1": [fb[6], fb[7]]}
        pools["b0"] = pools["B0"]
        pools["b1"] = pools["B1"]
        rr = {"A": 0, "B0": 0, "B1": 0}

        def psum(pool):
            bf = pool[0] == "b"
            pool = pool.upper()
            lst = pools[pool]
            b = lst[rr[pool] % len(lst)]
            rr[pool] += 1
            b.gen += 1
            v = V(b.v.ap, b.v.trk, b, b.gen)
            return v.bitcast(BF16) if bf else v

        ident = cst[:, 0, :]
        Ltri = cst[:, 1, :]
        SLm = cst[:, 2, :]
        blk = cst[:, 3, :]
        c0all = cst[:, 4, :]
        c1all = cst[:, 5, :]

        def prm_l(l, off, n):
            return prm[:, l * PL + off: l * PL + off + n]

        def b4(v):
            return v.un(2).bc([P, 4, P])

        def m4(v):
            return v.un(1).bc([P, 4, P])

        def f4(v):
            return v.re("p (a b) -> p a b", a=4)

        def fl(v):
            return v.re("p a b -> p (a b)")

        def rsq_pool(out, ps_in, eps):
            k.act(out, ps_in, AF.Ln, bias=eps)
            k.act(out, out, AF.Exp, scale=-0.5)

        k.dma("sp", prm, params_d)
        k.dma("sp", cst, consts_d)
        k.cp("dve", identb, ident)
        k.memset("pool", ones_mean, 1.0 / 1024.0)
        k.memset("pool", ones_h, 1.0 / 128.0)
        k.memset("pool", ones_q, 128.0)
        k.memset("pool", ones_k, 1.0)
        for l in range(NL):
            k.cp("dve", wabb[:, l, :, :], prm_l(l, 145, 128).re("p (a b) -> p a b", a=8))
            k.act(negA[:, l, :], prm_l(l, 129, 8), AF.Exp)
            k.ts("dve", negA[:, l, :], negA[:, l, :], -1.0, ALU.mult)
            k.ts("dve", ogh[:, l:l + 1], prm_l(l, 128, 1), 0.5, ALU.mult)
            k.ts("dve", scwh[:, l, :], prm_l(l, 104, 24), 0.5, ALU.mult)
            for h in range(2):
                k.memset("pool", Sf[l][h], 0.0)
                k.memset("pool", Sb[l][h], 0.0)
            k.memset("pool", tailq[l], 0.0)
            k.memset("pool", tails[l], 0.0)
        BPC = NBLK // NCH
        for l in range(NL):
            for c in range(NCH):
                k.dma("pool", wbf[l][c], wall[l, c * BPC:(c + 1) * BPC], nsem=4)

        stream = []
        for (t0, nt) in groups:
            for l in range(NL):
                for j in range(NBLK):
                    stream.append((l, j))
        wstate = {"issued": 0, "used": 0}

        def w_prefetch():
            while wstate["issued"] < len(stream) and wstate["issued"] < wstate["used"] + NWS:
                i = wstate["issued"]
                l, j = stream[i]
                k.dma("sp", wsl[i % NWS], wbf[l][j // BPC][j % BPC], nsem=NWS)
                wstate["issued"] += 1

        def w_next():
            w_prefetch()
            i = wstate["used"]
            wstate["used"] += 1
            return wsl[i % NWS]

        cnt = {"ev": 0, "pre": 0, "sq": 0, "sc": 0, "th": 0, "cols": 0}

        def evac_eng():
            cnt["ev"] += 1
            return "act" if cnt["ev"] % 2 else "dve"

        def silu2(dest, ps_in, G):
            th = tht[cnt["th"] % 2]
            cnt["th"] += 1
            k.act(th[:, :G], ps_in, AF.Tanh, scale=0.5)
            k.stt("dve", dest, th[:, :G], 1.0, ps_in, ALU.add, ALU.mult)

        def stream_A(l, G):
            cw = prm_l(l, 8, 96).re("p (c j) -> p c j", j=4)

            def stage1(p_, wblk, sub):
                st = {}
                if p_ < 24:
                    hfx = p_ // 12
                    which = (p_ % 12) // 4
                    h = p_ % 4
                    c = which * 8 + 4 * hfx + h
                    pr = pre[cnt["pre"] % 2]
                    dgs = dg[cnt["pre"] % 2]
                    cnt["pre"] += 1
                    k.cp("pool", pr[:, 0:3], tailq[l][:, c, :])
                    k.tt("pool", dgs, identb.un(1).bc([P, 4, P]), cw[:, c, :].un(2).bc([P, 4, P]), ALU.mult)
                    st.update(hfx=hfx, which=which, h=h, c=c, pr=pr, dgs=dgs)
                elif p_ >= 32 and (p_ - 32) % 4 == 3:
                    m = (p_ - 32) // 4
                    i2 = cnt["sc"] % 2
                    ps_ = preS[i2]
                    dgs = dg[cnt["pre"] % 2]
                    cnt["pre"] += 1
                    k.cp("pool", ps_[:, 0:2], tails[l][:, m, :])
                    k.tt("pool", dgs[:, 0:3, :], identb.un(1).bc([P, 3, P]),
                         scwh[:, l, 3 * m:3 * m + 3].un(2).bc([P, 3, P]), ALU.mult)
                    st.update(ps_=ps_, dgs=dgs)
                pp = psum("A")
                for kk in range(8):
                    k.mm(pp[:, :G], wblk[:, kk, sub * P:(sub + 1) * P], hT[:, kk, :G],
                         start=(kk == 0), stop=(kk == 7))
                st["pp"] = pp
                return st

            def stage2(p_, st):
                pp = st["pp"]
                if p_ < 24:
                    hfx, which, h, c, pr, dgs = st["hfx"], st["which"], st["h"], st["c"], st["pr"], st["dgs"]
                    k.cp(evac_eng(), pr[:, 3:3 + G], pp[:, :G])
                    k.cp("pool", tailq[l][:, c, :], pr[:, G:G + 3])
                    cp_ = psum("A")
                    for jj in range(4):
                        k.mm(cp_[:, :G], dgs[:, jj, :], pr[:, jj:jj + G], start=(jj == 0), stop=(jj == 3))
                    dest = qkv_h[hfx][:, which * 4 + h, :G]
                    silu2(dest, cp_[:, :G], G)
                    if which < 2:
                        s2 = sqs[cnt["sq"] % 2]
                        cnt["sq"] += 1
                        k.tt("pool", s2[:, :G], dest, dest, ALU.mult)
                        st["s2"] = s2
                elif p_ < 32:
                    hh = p_ - 24
                    silu2(zs_h[hh // 4][:, hh % 4, :G], pp[:, :G], G)
                else:
                    m = (p_ - 32) // 4
                    r = (p_ - 32) % 4
                    i2 = cnt["sc"] % 2
                    if r == 0:
                        silu2(szs[i2][:, :G], pp[:, :G], G)
                    elif r == 1:
                        k.tt("dve", bzs[i2][:, :G], pp[:, :G], szs[i2][:, :G], ALU.mult)
                    elif r == 2:
                        k.cp("act", csb[i2][:, :G], pp[:, :G])
                    else:
                        ps_, dgs = st["ps_"], st["dgs"]
                        k.tt("dve", ps_[:, 2:2 + G], pp[:, :G], csb[i2][:, :G], ALU.mult)
                        k.cp("pool", tails[l][:, m, :], ps_[:, G:G + 2])
                        cv = psum("A")
                        for jj in range(3):
                            k.mm(cv[:, :G], dgs[:, jj, :], ps_[:, jj:jj + G], start=(jj == 0), stop=(jj == 2))
                        k.tt("dve", mix_s[:, m, :G], cv[:, :G], bzs[i2][:, :G], ALU.mult)
                        cnt["sc"] += 1

            def stage3(p_, st):
                if p_ < 24 and st["which"] < 2:
                    hfx, which, h = st["hfx"], st["which"], st["h"]
                    np_ = psum("A")
                    k.mm(np_[:, :G], ones_q if which == 0 else ones_k, st["s2"][:, :G])
                    k.cp("dve", rb[h][:, :G], np_[:, :G])
                    if h == 3:
                        eps_ = (512.0 * EPS) if which == 0 else 4.0 * EPS
                        for h_ in range(4):
                            k.act(rb[h_][:, :G], rb[h_][:, :G], AF.Ln, bias=eps_)
                        for h_ in range(4):
                            k.act(rb[h_][:, :G], rb[h_][:, :G], AF.Exp, scale=-0.5)
                        for h_ in range(4):
                            dd = qkv_h[hfx][:, which * 4 + h_, :G]
                            k.tt("dve", dd, dd, rb[h_][:, :G], ALU.mult)
                cnt["cols"] += 1

            pend2 = None
            pend3 = None
            for j in range(32):
                wblk = w_next().re("p (k c) -> p k c", k=8)
                for sub in range(2):
                    p_ = 2 * j + sub
                    st = stage1(p_, wblk, sub)
                    if pend2 is not None:
                        stage2(*pend2)
                    if pend3 is not None:
                        stage3(*pend3)
                    pend3 = pend2
                    pend2 = (p_, st)
                    yield
            stage2(*pend2)
            stage3(*pend3)
            stage3(*pend2)
            yield

        def stream_B(l, hf, nt):
            d = H[hf]
            pf = "B%d" % hf
            pb16 = "b%d" % hf
            qh, kh, vh = q_h[hf], k_h[hf], v_h[hf]
            hs = slice(4 * hf, 4 * hf + 4)
            S_f, S_b = Sf[l][hf], Sb[l][hf]
            oT = oT_h[hf]
            for tau in range(nt):
                tc = slice(tau * P, (tau + 1) * P)

                def sc(name, lo=0):
                    return tsc[name][:, tau, lo + 4 * hf: lo + 4 * hf + 4]
                pk = psum(pb16)
                for h in range(4):
                    k.tr(pk[:, h * P:(h + 1) * P], kh[:, h, tc], identb)
                pkv = f4(pk[:, 0:512])
                k.tt("dve", d["Kbg"], pkv, b4(sc("kbg")), ALU.mult)
                k.tt("dve", d["Kdec"], pkv, b4(sc("ekd")), ALU.mult)
                yield
                pv_ = psum(pb16)
                for h in range(4):
                    k.tr(pv_[:, h * P:(h + 1) * P], vh[:, h, tc], identb)
                k.tt("dve", d["Vb"], f4(pv_[:, 0:512]), b4(sc("bh")), ALU.mult)
                yield
                k.tt("pool", d["rhsG"], m4(SLm), b4(sc("g")), ALU.mult)
                pD = psum(pf)
                k.mm(pD, Ltri, fl(d["rhsG"]))
                k.act(fl(d["E"]), pD, AF.Exp)
                yield
                k.tt("pool", d["rhsG2"], m4(Ltri), b4(sc("g")), ALU.mult)
                pDT = psum(pf)
                k.mm(pDT, SLm, fl(d["rhsG2"]))
                k.act(fl(d["ET"]), pDT, AF.Exp)
                yield
                k.tt("pool", d["nbm"], m4(SLm), b4(sc("nbeta")), ALU.mult)
                k.tt("pool", d["E"], d["E"], d["nbm"], ALU.mult)
                pKK = psum(pf)
                for h in range(4):
                    k.mm(pKK[:, h * P:(h + 1) * P], kh[:, h, tc], kh[:, h, tc])
                k.tt("dve", fl(d["B0"]), pKK, fl(d["E"]), ALU.mult)
                yield
                k.tt("pool", d["ET"], d["ET"], m4(Ltri), ALU.mult)
                pQK = psum(pf)
                for h in range(4):
                    k.mm(pQK[:, h * P:(h + 1) * P], kh[:, h, tc], qh[:, h, tc])
                k.tt("dve", fl(d["qkT"]), pQK, fl(d["ET"]), ALU.mult)
                yield
                k.tt("pool", d["dgE"], m4(identb), b4(sc("exps")), ALU.mult)
                pe_ = psum(pf)
                k.mm(pe_, ones_k, fl(d["dgE"]))
                k.tt("dve", d["qd"], qh[:, :, tc], f4(pe_), ALU.mult)
                yield
                pU = psum(pb16)
                for h in range(4):
                    k.tr(pU[:, h * P:(h + 1) * P], d["B0"][:, h, :], identb)
                UX, UXn = d["UXa"], d["UXb"]
                Bk, Bkn = d["Bka"], d["Bkb"]
                k.cp("act", UX[:, :, 0, :], f4(pU[:, 0:512]))
                yield
                p1 = psum(pf)
                for h in range(4):
                    k.mm(p1[:, h * P:(h + 1) * P], d["B0"][:, h, :], UX[:, h, 0, :])
                k.cp("act", UXn[:, :, 0, :], f4(p1))
                k.tt("dve", UXn[:, :, 1, :], UX[:, :, 0, :], m4(identb), ALU.add)
                yield
                p2 = psum(pf)
                for h in range(4):
                    k.mm(p2[:, h * P:(h + 1) * P], UX[:, h, 0, :], d["B0"][:, h, :])
                k.cp("act", Bkn, f4(p2))
                UX, UXn = UXn, UX
                Bk, Bkn = Bkn, Bk
                yield
                for lev in range(1, 4):
                    for hp in range(2):
                        pa = psum(pf)
                        for h2 in range(2):
                            h = 2 * hp + h2
                            k.mm(pa[:, h2 * 256:h2 * 256 + 256], Bk[:, h, :], UX[:, h, :, :].re("p a b -> p (a b)"))
                        pv = pa.re("p (a t b) -> p a t b", a=2, t=2)
                        k.cp("act", UXn[:, 2 * hp:2 * hp + 2, 0, :], pv[:, :, 0, :])
                        k.tt("dve", UXn[:, 2 * hp:2 * hp + 2, 1, :], UX[:, 2 * hp:2 * hp + 2, 1, :],
                             pv[:, :, 1, :], ALU.add)
                        yield
                    pb_ = psum(pf)
                    for h in range(4):
                        k.mm(pb_[:, h * P:(h + 1) * P], UX[:, h, 0, :], Bk[:, h, :])
                    k.cp("act", Bkn, f4(pb_))
                    UX, UXn = UXn, UX
                    Bk, Bkn = Bkn, Bk
                    yield
                px = psum(pf)
                for h in range(4):
                    k.mm(px[:, h * P:(h + 1) * P], Bk[:, h, :], UX[:, h, 1, :])
                k.tt("dve", UXn[:, :, 1, :], UX[:, :, 1, :], f4(px), ALU.add)
                yield
                pb_ = psum(pf)
                for h in range(4):
                    k.mm(pb_[:, h * P:(h + 1) * P], UX[:, h, 0, :], Bk[:, h, :])
                k.cp("act", Bkn, f4(pb_))
                UX, UXn = UXn, UX
                Bk, Bkn = Bkn, Bk
                yield
                px = psum(pf)
                for h in range(4):
                    k.mm(px[:, h * P:(h + 1) * P], Bk[:, h, :], UX[:, h, 1, :])
                k.tt("dve", d["XT"], UX[:, :, 1, :], f4(px), ALU.add)
                yield
                pu = psum(pf)
                for h in range(4):
                    k.mm(pu[:, h * P:(h + 1) * P], d["XT"][:, h, :], d["Vb"][:, h, :])
                k.cp("act", d["u"], f4(pu))
                yield
                pw = psum(pf)
                for h in range(4):
                    k.mm(pw[:, h * P:(h + 1) * P], d["Kbg"][:, h, :], d["XT"][:, h, :])
                k.cp("act", d["wT"], f4(pw))
                yield
                for c in range(2):
                    cs = slice(64 * c, 64 * c + 64)
                    pws = psum(pf)
                    for h in range(4):
                        k.mm(pws[cs, h * P:(h + 1) * P], d["wT"][:, h, cs], S_b[:, h, :])
                    k.tt("dve", d["vnew"][cs, :, :], d["u"][cs, :, :], f4(pws[cs, :]), ALU.subtract)
                    yield
                    po = psum(pf)
                    for h in range(4):
                        k.mm(po[:, h * 64:(h + 1) * 64], S_b[:, h, :], d["qd"][:, h, cs], start=True, stop=False)
                        k.mm(po[:, h * 64:(h + 1) * 64], d["vnew"][cs, h, :], d["qkT"][cs, h, cs],
                             start=False, stop=True)
                    k.cp("act", oT[:, :, tau * P + 64 * c: tau * P + 64 * c + 64],
                         po[:, 0:256].re("p (a b) -> p a b", a=4))
                    yield
                    pdS = psum(pf)
                    for h in range(4):
                        k.mm(pdS[:, h * P:(h + 1) * P], d["Kdec"][cs, h, :], d["vnew"][cs, h, :])
                    k.tt("pool", d["tmpS"], S_f, b4(sc("exps", 16 + 8 * c)), ALU.mult)
                    k.tt("dve", S_f, d["tmpS"], f4(pdS), ALU.add)
                    k.cp("act", S_b, S_f)
                    yield

        xt_i = 0
        ot_i = 0
        for (t0, nt) in groups:
            G = nt * P
            for tau in range(nt):
                xt = xtile[xt_i % 2]
                xt_i += 1
                k.dma("sp", xt, xin[(t0 + tau) * P:(t0 + tau + 1) * P, :])
                for half in range(2):
                    pt = psum("A")
                    for f in range(4):
                        ff = half * 4 + f
                        k.tr(pt[:, f * P:(f + 1) * P], xt[:, ff * P:(ff + 1) * P], ident)
                    k.cp(evac_eng(), xT[:, half * 4:half * 4 + 4, tau * P:(tau + 1) * P], f4(pt))
            for l in range(NL):
                g_l = prm_l(l, 0, 8)
                dtb = prm_l(l, 137, 8)
                sq = mix_s[:, :, :G]
                k.act(sq, xT[:, :, :G], AF.Square)
                pn = psum("A")
                for kk in range(8):
                    k.mm(pn[:, :G], ones_mean, sq[:, kk, :], start=(kk == 0), stop=(kk == 7))
                rsq_pool(rstd[:, :G], pn[:, :G], EPS)
                for kk in range(8):
                    k.stt("dve", hT[:, kk, :G], xT[:, kk, :G], g_l[:, kk:kk + 1], rstd[:, :G], ALU.mult, ALU.mult)
                pab = psum("A")
                for tau in range(nt):
                    for kk in range(8):
                        k.mm(pab[:, tau * 16:(tau + 1) * 16], hT[:, kk, tau * P:(tau + 1) * P], wabb[:, l, kk, :],
                             start=(kk == 0), stop=(kk == 7))
                pabv = pab[:, 0:nt * 16].re("p (t w) -> p t w", w=16)

                def T_(n):
                    return tsc[n][:, :nt, :]
                k.tt("dve", T_("t1"), pabv[:, :, 0:8], dtb.un(1).bc([P, nt, 8]), ALU.add)
                k.act(T_("e2"), pabv[:, :, 8:16], AF.Exp, scale=-1.0)
                k.act(T_("e1"), T_("t1"), AF.Exp)
                k.act(T_("sp"), T_("e1"), AF.Ln, bias=1.0)
                k.tt("dve", T_("g"), T_("sp"), negA[:, l, :].un(1).bc([P, nt, 8]), ALU.mult)
                k.ts("dve", T_("e2"), T_("e2"), 1.0, ALU.add)
                k.recip(T_("beta"), T_("e2"))
                k.ts("dve", T_("nbeta"), T_("beta"), -1.0, ALU.mult)
                k.ts("dve", T_("bh"), T_("beta"), 0.5, ALU.mult)
                pcs = psum("A")
                for tau in range(nt):
                    for ii, mat in enumerate((Ltri, blk, c0all, c1all)):
                        k.mm(pcs[:, tau * 32 + ii * 8: tau * 32 + ii * 8 + 8], mat, tsc["g"][:, tau, :])
                k.cp("dve", T_("gcs"), pcs[:, 0:nt * 32].re("p (t w) -> p t w", w=32))
                k.tt("dve", T_("dif"), T_("gcs")[:, :, 8:16], T_("gcs")[:, :, 0:8], ALU.subtract)
                k.act(T_("exps"), T_("gcs"), AF.Exp)
                k.act(T_("ekd"), T_("dif"), AF.Exp)
                k.tt("dve", T_("kbg"), T_("beta"), T_("exps")[:, :, 0:8], ALU.mult)
                cnt["cols"] = 0
                A = stream_A(l, G)
                Bs = [stream_B(l, 0, nt), stream_B(l, 1, nt)]
                started = [False, False]
                active = [A]
                while active:
                    for g_ in list(active):
                        try:
                            next(g_)
                        except StopIteration:
                            active.remove(g_)
                    if not started[0] and cnt["cols"] >= 12:
                        active.append(Bs[0])
                        started[0] = True
                    if not started[1] and cnt["cols"] >= 24:
                        active.append(Bs[1])
                        started[1] = True
                for hf in range(2):
                    osq = qkv_h[hf][:, 0:4, :G]
                    k.act(osq, oT_h[hf][:, :, :G], AF.Square)
                    pns = []
                    for h in range(4):
                        pn2 = psum("B%d" % hf)
                        k.mm(pn2[:, :G], ones_h, osq[:, h, :])
                        k.cp("dve", rb[h][:, :G], pn2[:, :G])
                    for h in range(4):
                        k.act(rb[h][:, :G], rb[h][:, :G], AF.Ln, bias=EPS)
                    for h in range(4):
                        k.act(rb[h][:, :G], rb[h][:, :G], AF.Exp, scale=-0.5)
                    for h in range(4):
                        k.stt("dve", oT_h[hf][:, h, :G], oT_h[hf][:, h, :G], ogh[:, l:l + 1], rb[h][:, :G],
                              ALU.mult, ALU.mult)
                        k.tt("pool", mix_h[hf][:, h, :G], oT_h[hf][:, h, :G], zs_h[hf][:, h, :G], ALU.mult)
                for f in range(8):
                    wblk = w_next().re("p (m c) -> p m c", m=16)
                    po = psum("A")
                    for m in range(16):
                        rhs = mix_h[0][:, m, :G] if m < 4 else (mix_h[1][:, m - 4, :G] if m < 8 else mix_s[:, m - 8, :G])
                        k.mm(po[:, :G], wblk[:, m, :], rhs, start=(m == 0), stop=(m == 15))
                    k.tt("dve", xT[:, f, :G], xT[:, f, :G], po[:, :G], ALU.add)
            if final:
                fg = prm[:, NL * PL: NL * PL + 8]
                sq = mix_s[:, :, :G]
                k.act(sq, xT[:, :, :G], AF.Square)
                pn = psum("A")
                for kk in range(8):
                    k.mm(pn[:, :G], ones_mean, sq[:, kk, :], start=(kk == 0), stop=(kk == 7))
                rsq_pool(rstd[:, :G], pn[:, :G], EPS)
                for kk in range(8):
                    k.stt("dve", xT[:, kk, :G], xT[:, kk, :G], fg[:, kk:kk + 1], rstd[:, :G], ALU.mult, ALU.mult)
            for tau in range(nt):
                gt = t0 + tau
                if final and gt == 0:
                    continue
                ot = otile[ot_i % 2]
                ot_i += 1
                for half in range(2):
                    pt = psum("A")
                    for f in range(4):
                        ff = half * 4 + f
                        k.tr(pt[:, f * P:(f + 1) * P], xT[:, ff, tau * P:(tau + 1) * P], ident)
                    k.cp(evac_eng(), ot[:, half * 512:(half + 1) * 512], pt)
                orow = (gt - 1) if final else gt
                k.dma("sp", out_d[orow * P:(orow + 1) * P, :], ot)
        k.finish([out_d])
    return nc


def _consts():
    i = np.arange(P)
    same = (i[:, None] // 64) == (i[None, :] // 64)
    c = np.zeros((P, 6, P), np.float32)
    c[:, 0, :] = np.eye(P)
    c[:, 1, :] = same & (i[:, None] <= i[None, :])
    c[:, 2, :] = same & (i[:, None] > i[None, :])
    c[:, 3, :] = same
    c[:, 4, :] = (i[:, None] < 64)
    c[:, 5, :] = (i[:, None] >= 64)
    return c


def _layout_weights(w_in, w_out, layers):
    NL = len(layers)
    wall = np.empty((NL, NBLK, P, 2048), np.float32)
    wab = np.empty((NL, P, 128), np.float32)
    DN = 1024
    for n, l in enumerate(layers):
        W = w_in[l]
        qkv = W[:, 0:3 * DN]
        z = W[:, 3 * DN:4 * DN]
        ab = W[:, 4 * DN:4 * DN + 16]
        o = 4 * DN + 16
        scb = W[:, o:o + 1024]
        scc = W[:, o + 1024:o + 2048]
        sch = W[:, o + 2048:o + 3072]
        scz = W[:, o + 3072:o + 4096]
        sc = np.stack([scz.reshape(1024, 8, 128), scb.reshape(1024, 8, 128), scc.reshape(1024, 8, 128),
                       sch.reshape(1024, 8, 128)], axis=2).reshape(1024, 4096)
        q_, k_, v_ = qkv[:, 0:DN], qkv[:, DN:2 * DN], qkv[:, 2 * DN:3 * DN]
        dn = np.concatenate([q_[:, 0:512], k_[:, 0:512], v_[:, 0:512], q_[:, 512:], k_[:, 512:], v_[:, 512:]], axis=1)
        main = np.concatenate([dn, z, sc], axis=1)
        wall[n, 0:32] = main.reshape(8, P, 32, 256).transpose(2, 1, 0, 3).reshape(32, P, 2048)
        wall[n, 32:40] = w_out[l].reshape(16, P, 8, 128).transpose(2, 1, 0, 3).reshape(8, P, 2048)
        wab[n] = ab.reshape(8, P, 16).transpose(1, 0, 2).reshape(P, 128)
    return wall, wab


def _layout_params(norm_g, dn_conv_w, dn_A_log, dn_dt_bias, dn_out_g, sc_conv_w, final_g, wab, layers):
    NL = len(layers)
    prm = np.zeros((P, NL * PL + 8), np.float32)
    for n, l in enumerate(layers):
        b = n * PL
        prm[:, b:b + 8] = norm_g[l].reshape(8, P).T
        prm[:, b + 8:b + 104] = dn_conv_w[l].reshape(4, 24, P).transpose(2, 1, 0).reshape(P, 96)
        prm[:, b + 104:b + 128] = sc_conv_w[l].reshape(3, 8, P).transpose(2, 1, 0).reshape(P, 24)
        prm[:, b + 128] = dn_out_g[l]
        prm[:, b + 129:b + 137] = np.broadcast_to(dn_A_log[l][None, :], (P, 8))
        prm[:, b + 137:b + 145] = np.broadcast_to(dn_dt_bias[l][None, :], (P, 8))
        prm[:, b + 145:b + 273] = wab[n]
    prm[:, NL * PL:NL * PL + 8] = final_g.reshape(8, P).T
    return prm


_CACHE = {}


def _get_program(T, groups, NL, first, final):
    key = (T, tuple(groups), NL, first, final)
    if key not in _CACHE:
        _CACHE[key] = build_program(T, groups, NL, first, final)
    return _CACHE[key]


def make_groups(T, nt):
    groups = []
    t = 0
    while t < T:
        n = min(nt, T - t)
        groups.append((t, n))
        t += n
    return groups


FUSED = True
GROUP_NT = 3


def kernel(x, meta_tokens, norm_g, w_in, dn_conv_w, dn_A_log, dn_dt_bias, dn_out_g, sc_conv_w, w_out, final_g):
    x = np.asarray(x, np.float32)
    B, SEQ, D = x.shape
    T = (SEQ + P) // P
    assert (T - 1) * P == SEQ
    xin = np.zeros((B, T * P, D), np.float32)
    xin[:, P - N_META:P, :] = np.asarray(meta_tokens, np.float32)[None]
    xin[:, P:, :] = x
    consts = _consts()
    groups = make_groups(T, GROUP_NT)
    args = [np.asarray(a, np.float32) for a in (norm_g, dn_conv_w, dn_A_log, dn_dt_bias, dn_out_g, sc_conv_w, final_g)]
    w_in = np.asarray(w_in, np.float32)
    w_out = np.asarray(w_out, np.float32)
    plan = [list(range(DEPTH))] if FUSED else [[l] for l in range(DEPTH)]
    cur = xin
    for pi, layers in enumerate(plan):
        first = pi == 0
        final = pi == len(plan) - 1
        wall, wab = _layout_weights(w_in, w_out, layers)
        prm = _layout_params(*args, wab, layers)
        nc = _get_program(T, groups, len(layers), first, final)
        in_maps = [{"xin": np.ascontiguousarray(cur[b]), "wall": wall, "params": prm, "consts": consts}
                   for b in range(B)]
        res = run_bass_kernel_spmd(nc, in_maps, core_ids=list(range(B)))
        cur = np.stack([np.asarray(r["out"]) for r in res.results], axis=0)
    return cur.astype(np.float32)
```
